# Optimizing a Trainium2 kernel written in Bass

```python
import math
import jax
import jax.numpy as jnp
from jax import lax
import numpy as np

D_MODEL = 1024
BATCH = 2
SEQ = 16384
DEPTH = 2

HEAD_DIM = 64
N_MIX_HEADS = D_MODEL // HEAD_DIM
GLA_HEADS = N_MIX_HEADS // 4
MOBA_HEADS = N_MIX_HEADS // 2
RWKV_HEADS = N_MIX_HEADS - GLA_HEADS - MOBA_HEADS
GLA_W = GLA_HEADS * HEAD_DIM
MOBA_W = MOBA_HEADS * HEAD_DIM
RWKV_W = RWKV_HEADS * HEAD_DIM
D_MIX = GLA_W + MOBA_W + RWKV_W

GLA_LOWRANK = 16
GLA_TAU = 16.0
GLA_CHUNK = 64

MOBA_BLOCK = 256
MOBA_TOPK = 3
Q_CHUNK = 128

RWKV_LORA = 32
RWKV_GN_EPS = 64e-5

N_BUCKETS = 32
MAX_DISTANCE = 4096
LN_EPS = 1e-5

DN_ALPHA = (2.0 * DEPTH) ** 0.25
DN_BETA = (8.0 * DEPTH) ** -0.25

GLA_COLS = 4 * GLA_W + GLA_LOWRANK
MOBA_COLS = 4 * MOBA_W
RWKV_COLS = 4 * RWKV_W + 2 * RWKV_LORA
D_IN = GLA_COLS + MOBA_COLS + RWKV_COLS

kernel_name = "hymba_style_gla_moba_rwkv7_deepnorm"


def _split(t, sizes):
    idx = [int(v) for v in np.cumsum(sizes)[:-1]]
    return jnp.split(t, idx, axis=-1)


def _to_heads(t, n_heads):
    b, s, _ = t.shape
    return t.reshape(b, s, n_heads, HEAD_DIM).transpose(0, 2, 1, 3)


def _from_heads(t):
    b, n, s, d = t.shape
    return t.transpose(0, 2, 1, 3).reshape(b, s, n * d)


def _layer_norm(x, w, b):
    xf = x.astype(jnp.float32)
    mu = jnp.mean(xf, axis=-1, keepdims=True)
    var = jnp.mean(jnp.square(xf - mu), axis=-1, keepdims=True)
    return (xf - mu) * lax.rsqrt(var + LN_EPS) * w + b


def _t5_bucket(rel):
    rel = jnp.maximum(rel, 0)
    max_exact = N_BUCKETS // 2
    rel_f = jnp.maximum(rel, 1).astype(jnp.float32)
    large = max_exact + (jnp.log(rel_f / max_exact) / math.log(MAX_DISTANCE / max_exact)
                         * (N_BUCKETS - max_exact)).astype(jnp.int32)
    large = jnp.minimum(large, N_BUCKETS - 1)
    return jnp.where(rel < max_exact, rel, large)


def _gla_chunked(q, k, v, log_a):
    b, h, s, dk = q.shape
    dv = v.shape[-1]
    n = s // GLA_CHUNK

    def chunks(t):
        return t.reshape(b, h, n, GLA_CHUNK, t.shape[-1]).transpose(2, 0, 1, 3, 4)

    causal = jnp.tril(jnp.ones((GLA_CHUNK, GLA_CHUNK), dtype=bool))[:, :, None]

    def step(state, inp):
        qc, kc, vc, gc = inp
        cum = jnp.cumsum(gc, axis=-2)
        o_inter = jnp.einsum('bhtk,bhkv->bhtv', qc * jnp.exp(cum), state)
        diff = cum[:, :, :, None, :] - cum[:, :, None, :, :]
        decay = jnp.exp(jnp.where(causal, diff, -jnp.inf))
        scores = jnp.einsum('bhtk,bhsk,bhtsk->bhts', qc, kc, decay)
        o_intra = jnp.einsum('bhts,bhsv->bhtv', scores, vc)
        last = cum[:, :, -1:, :]
        state = state * jnp.exp(last[:, :, 0, :])[..., None] + jnp.einsum(
            'bhsk,bhsv->bhkv', kc * jnp.exp(last - cum), vc)
        return state, o_inter + o_intra

    s0 = jnp.zeros((b, h, dk, dv), jnp.float32)
    _, out = lax.scan(step, s0, (chunks(q), chunks(k), chunks(v), chunks(log_a)))
    return out.transpose(1, 2, 0, 3, 4).reshape(b, h, s, dv)


def _gla_branch(hg, a_up, a_bias, norm_w):
    hg = hg.astype(jnp.float32)
    q, k, v, g, a_dn = _split(hg, [GLA_W] * 4 + [GLA_LOWRANK])
    log_a = jax.nn.log_sigmoid(a_dn @ a_up.astype(jnp.float32) + a_bias) / GLA_TAU
    o = _gla_chunked(_to_heads(q, GLA_HEADS) * HEAD_DIM ** -0.5, _to_heads(k, GLA_HEADS),
                     _to_heads(v, GLA_HEADS), _to_heads(log_a, GLA_HEADS))
    o = o * lax.rsqrt(jnp.mean(jnp.square(o), axis=-1, keepdims=True) + LN_EPS) * norm_w
    return _from_heads(o) * jax.nn.silu(g)


def _moba_attention(q, k, v, rel_bias):
    b, h, s, dh = q.shape
    nb = -(-s // MOBA_BLOCK)
    pad = nb * MOBA_BLOCK - s
    kp = jnp.pad(k, ((0, 0), (0, 0), (0, pad), (0, 0))).reshape(b, h, nb, MOBA_BLOCK, dh)
    vp = jnp.pad(v, ((0, 0), (0, 0), (0, pad), (0, 0))).reshape(b, h, nb, MOBA_BLOCK, dh)
    kbar = jnp.mean(kp, axis=3)
    top = min(MOBA_TOPK, nb)
    nq = s // Q_CHUNK
    q_chunks = q.reshape(b, h, nq, Q_CHUNK, dh).transpose(2, 0, 1, 3, 4)
    bias_tab = rel_bias.T.astype(jnp.float32)
    b_idx = jnp.arange(b)[:, None, None, None]
    h_idx = jnp.arange(h)[None, :, None, None]
    blk_ar = jnp.arange(MOBA_BLOCK)
    blocks = jnp.arange(nb)
    scale = dh ** -0.5

    def one_chunk(args):
        qi, ci = args
        q_pos = ci * Q_CHUNK + jnp.arange(Q_CHUNK)
        own = (ci * Q_CHUNK) // MOBA_BLOCK
        gate = jnp.einsum('bhqd,bhnd->bhqn', qi, kbar)
        gate = jnp.where(blocks < own, gate, -jnp.inf)
        _, idx = lax.top_k(gate, top)
        valid = idx < own
        k_sel = kp[b_idx, h_idx, idx]
        v_sel = vp[b_idx, h_idx, idx]
        kpos_sel = idx[..., None] * MOBA_BLOCK + blk_ar
        bias_sel = bias_tab[h_idx[..., None], _t5_bucket(q_pos[:, None, None] - kpos_sel)]
        s_sel = jnp.einsum('bhqd,bhqjkd->bhqjk', qi, k_sel) * scale + bias_sel
        s_sel = jnp.where(valid[..., None], s_sel, -jnp.inf)
        k_own = lax.dynamic_index_in_dim(kp, own, axis=2, keepdims=False)
        v_own = lax.dynamic_index_in_dim(vp, own, axis=2, keepdims=False)
        rel_own = q_pos[:, None] - (own * MOBA_BLOCK + blk_ar)[None, :]
        s_own = jnp.einsum('bhqd,bhkd->bhqk', qi, k_own) * scale + bias_tab[:, _t5_bucket(rel_own)]
        s_own = jnp.where(rel_own >= 0, s_own, -jnp.inf)
        logits = jnp.concatenate([s_sel.reshape(b, h, Q_CHUNK, top * MOBA_BLOCK), s_own], axis=-1)
        p = jax.nn.softmax(logits, axis=-1)
        p_sel = p[..., :top * MOBA_BLOCK].reshape(b, h, Q_CHUNK, top, MOBA_BLOCK)
        p_own = p[..., top * MOBA_BLOCK:]
        return (jnp.einsum('bhqjk,bhqjkd->bhqd', p_sel, v_sel)
                + jnp.einsum('bhqk,bhkd->bhqd', p_own, v_own))

    out = lax.map(one_chunk, (q_chunks, jnp.arange(nq)))
    return out.transpose(1, 2, 0, 3, 4).reshape(b, h, s, dh)


def _moba_branch(hm, rel_bias):
    hm = hm.astype(jnp.float32)
    q, k, v, g = _split(hm, [MOBA_W] * 4)
    o = _moba_attention(_to_heads(q, MOBA_HEADS), _to_heads(k, MOBA_HEADS),
                        _to_heads(v, MOBA_HEADS), rel_bias)
    return _from_heads(o) * jax.nn.silu(g)


def _rwkv7_scan(r, w, k, v, aa, bb):
    b, s, h, d = r.shape

    def step(state, inp):
        rt, wt, kt, vt, at, bt = inp
        sa = jnp.einsum('bhvk,bhk->bhv', state, at)
        state = state * wt[:, :, None, :] + sa[..., None] * bt[:, :, None, :] + vt[..., None] * kt[:, :, None, :]
        return state, jnp.einsum('bhvk,bhk->bhv', state, rt)

    xs = tuple(jnp.moveaxis(t, 1, 0) for t in (r, w, k, v, aa, bb))
    _, y = lax.scan(step, jnp.zeros((b, h, d, d), jnp.float32), xs)
    return jnp.moveaxis(y, 0, 1)


def _rwkv_branch(hr, mu, w_up, w0, a_up, a0, k_k, k_a, r_k, gn_w, gn_b):
    b, s, _ = hr.shape
    f32 = jnp.float32
    p = hr.astype(f32)
    p_prev = jnp.pad(p, ((0, 0), (1, 0), (0, 0)))[:, :-1]
    p = p + (p_prev - p) * mu
    r, k, v, g, w_dn, a_dn = _split(p, [RWKV_W] * 4 + [RWKV_LORA] * 2)
    d = w0 + jnp.tanh(w_dn) @ w_up.astype(f32)
    decay = jnp.exp(-jnp.exp(-jax.nn.softplus(-d) - 0.5))
    a = jax.nn.sigmoid(a0 + a_dn @ a_up.astype(f32))
    hs = lambda t: t.reshape(b, s, RWKV_HEADS, HEAD_DIM)
    hv = lambda t: t.reshape(RWKV_HEADS, HEAD_DIM)
    kk = hs(k * k_k)
    kk = kk / jnp.maximum(jnp.sqrt(jnp.sum(jnp.square(kk), axis=-1, keepdims=True)), 1e-12)
    k = k * (1.0 + (a - 1.0) * k_a)
    r_h, w_h, k_h, v_h, a_h = hs(r), hs(decay), hs(k), hs(v), hs(a)
    y = _rwkv7_scan(r_h, w_h, k_h, v_h, -kk, kk * a_h)
    ym = jnp.mean(y, axis=-1, keepdims=True)
    yv = jnp.mean(jnp.square(y - ym), axis=-1, keepdims=True)
    y = (y - ym) * lax.rsqrt(yv + RWKV_GN_EPS) * hv(gn_w) + hv(gn_b)
    y = y + jnp.sum(r_h * k_h * hv(r_k), axis=-1, keepdims=True) * v_h
    return y.reshape(b, s, RWKV_W) * jax.nn.silu(g)


def _layer(x, w_in, w_out, gla_a_up, gla_a_bias, gla_norm_w, rel_bias, mu, w_up, w0,
           a_up, a0, k_k, k_a, r_k, gn_w, gn_b, ln_w, ln_b):
    h = jnp.einsum('btd,de->bte', x, w_in)
    h_gla, h_moba, h_rwkv = _split(h, [GLA_COLS, MOBA_COLS, RWKV_COLS])
    o = jnp.concatenate([
        _gla_branch(h_gla, gla_a_up, gla_a_bias, gla_norm_w),
        _moba_branch(h_moba, rel_bias),
        _rwkv_branch(h_rwkv, mu, w_up, w0, a_up, a0, k_k, k_a, r_k, gn_w, gn_b),
    ], axis=-1)
    y = jnp.einsum('bte,ed->btd', o, w_out.astype(jnp.float32))
    return _layer_norm(DN_ALPHA * x.astype(jnp.float32) + y, ln_w, ln_b).astype(x.dtype)


def setup_inputs(seed: int = 0) -> dict:
    key = jax.random.key(seed)
    ks = jax.random.split(key, 20)
    L = DEPTH
    f32 = jnp.float32
    nrm = lambda k_, shp: jax.random.normal(k_, shp, f32)
    col_scale = jnp.ones((D_IN,), f32)
    for start, width in ((2 * GLA_W, GLA_W), (GLA_COLS + 2 * MOBA_W, MOBA_W),
                         (GLA_COLS + MOBA_COLS + 2 * RWKV_W, RWKV_W)):
        col_scale = col_scale.at[start:start + width].set(DN_BETA)
    return {
        "x": nrm(ks[0], (BATCH, SEQ, D_MODEL)),
        "w_in": nrm(ks[1], (L, D_MODEL, D_IN)) * D_MODEL ** -0.5 * col_scale,
        "w_out": nrm(ks[2], (L, D_MIX, D_MODEL)) * D_MIX ** -0.5 * DN_BETA,
        "gla_a_up": nrm(ks[3], (L, GLA_LOWRANK, GLA_W)) * GLA_LOWRANK ** -0.5,
        "gla_a_bias": nrm(ks[4], (L, GLA_W)) * 0.5,
        "gla_norm_w": 1.0 + 0.05 * nrm(ks[5], (L, HEAD_DIM)),
        "moba_rel_bias": nrm(ks[6], (N_BUCKETS, MOBA_HEADS)) * 0.5,
        "rwkv_mu": jax.random.uniform(ks[7], (L, RWKV_COLS), f32),
        "rwkv_w_up": nrm(ks[8], (L, RWKV_LORA, RWKV_W)) * 0.1,
        "rwkv_w0": jax.random.uniform(ks[9], (L, RWKV_W), f32, -6.0, 1.0),
        "rwkv_a_up": nrm(ks[10], (L, RWKV_LORA, RWKV_W)) * 0.5 * RWKV_LORA ** -0.5,
        "rwkv_a0": nrm(ks[11], (L, RWKV_W)) * 0.5,
        "rwkv_k_k": 0.85 + 0.05 * nrm(ks[12], (L, RWKV_W)),
        "rwkv_k_a": 1.0 + 0.05 * nrm(ks[13], (L, RWKV_W)),
        "rwkv_r_k": nrm(ks[14], (L, RWKV_W)) * 0.1,
        "rwkv_gn_w": 1.0 + 0.05 * nrm(ks[15], (L, RWKV_W)),
        "rwkv_gn_b": 0.02 * nrm(ks[16], (L, RWKV_W)),
        "ln_w": 1.0 + 0.05 * nrm(ks[17], (L, D_MODEL)),
        "ln_b": 0.02 * nrm(ks[18], (L, D_MODEL)),
    }


def reference(x, w_in, w_out, gla_a_up, gla_a_bias, gla_norm_w, moba_rel_bias, rwkv_mu,
              rwkv_w_up, rwkv_w0, rwkv_a_up, rwkv_a0, rwkv_k_k, rwkv_k_a, rwkv_r_k,
              rwkv_gn_w, rwkv_gn_b, ln_w, ln_b):
    for l in range(DEPTH):
        x = _layer(x, w_in[l], w_out[l], gla_a_up[l], gla_a_bias[l], gla_norm_w[l], moba_rel_bias,
                   rwkv_mu[l], rwkv_w_up[l], rwkv_w0[l], rwkv_a_up[l], rwkv_a0[l], rwkv_k_k[l],
                   rwkv_k_a[l], rwkv_r_k[l], rwkv_gn_w[l], rwkv_gn_b[l], ln_w[l], ln_b[l])
    return x
```

```python
import numpy as np
from contextlib import ExitStack
import concourse.bass as bass
import concourse.mybir as mybir
from concourse.bass_utils import run_bass_kernel_spmd

F32 = mybir.dt.float32
BF16 = mybir.dt.bfloat16
ALU = mybir.AluOpType
AF = mybir.ActivationFunctionType
AX = mybir.AxisListType

D_MODEL = 1024
BATCH = 2
SEQ = 16384
DEPTH = 2
HD = 64
LN_EPS = 1e-5
DN_ALPHA = (2.0 * DEPTH) ** 0.25
NCORES = 8

SAME_ENGINE_SYNC = True


class Prog:
    ENGS = ("pe", "act", "dve", "pool", "sp")

    def __init__(self, nc, n_dma_sems=32):
        self.nc = nc
        self.es = ExitStack()
        self.sem = {e: self.es.enter_context(nc.semaphore("s_" + e)) for e in self.ENGS}
        self.dsem = [self.es.enter_context(nc.semaphore("d%d" % i)) for i in range(n_dma_sems)]
        self.dcnt = [0] * n_dma_sems
        self.dnext = 0
        self.cnt = {e: 0 for e in self.ENGS}
        self.stream = {e: [] for e in self.ENGS}
        self.seen = {e: {} for e in self.ENGS}
        self.lastw = {}
        self.readers = {}
        self.out_tokens = []
        self.n_ps = 0

    def sb(self, name, shape, dt=F32):
        return self.es.enter_context(self.nc.sbuf_tensor(name, list(shape), dt))

    def ps(self, name, shape, dt=F32):
        return self.es.enter_context(self.nc.psum_tensor(name, list(shape), dt))

    def _deps(self, reads, writes):
        deps = []
        for k in reads:
            t = self.lastw.get(k)
            if t is not None:
                deps.append(t)
        for k in writes:
            t = self.lastw.get(k)
            if t is not None:
                deps.append(t)
            deps.extend(self.readers.get(k, ()))
        return deps

    def _commit(self, tok, reads, writes):
        for k in reads:
            self.readers.setdefault(k, []).append(tok)
        for k in writes:
            self.lastw[k] = tok
            self.readers[k] = []

    def _waits(self, eng, deps, is_dma):
        best = {}
        for (key, val) in deps:
            if (not is_dma) and key == eng and (eng == "pe" or not SAME_ENGINE_SYNC):
                continue
            if self.seen[eng].get(key, 0) >= val:
                continue
            if best.get(key, 0) < val:
                best[key] = val
        for key, val in best.items():
            self.seen[eng][key] = val
        return list(best.items())

    def _semh(self, key):
        return self.sem[key] if isinstance(key, str) else self.dsem[key]

    PSUM_KEYS = ("psm", "pb", "pB", "ps", "pS", "pO")

    def op(self, eng, fn, reads=(), writes=()):
        pr_ = [k for k in reads if isinstance(k, tuple) and k[0] in self.PSUM_KEYS]
        if pr_:
            reads = [k for k in reads if k not in pr_]
            writes = list(writes) + pr_
        deps = self._deps(reads, writes)
        waits = self._waits(eng, deps, False)
        self.cnt[eng] += 1
        tok = (eng, self.cnt[eng])
        self.stream[eng].append((fn, waits, self.sem[eng], 1))
        self._commit(tok, reads, writes)
        return tok

    def dma(self, eng, out, in_, reads=(), writes=(), is_output=False, **kw):
        assert eng in ("sp", "act")
        deps = self._deps(reads, writes)
        waits = self._waits(eng, deps, True)
        i = self.dnext
        self.dnext = (self.dnext + 1) % len(self.dsem)
        self.dcnt[i] += 16
        tok = (i, self.dcnt[i])
        self.stream[eng].append((lambda e: e.dma_start(out=out, in_=in_, **kw), waits, self.dsem[i], 16))
        self._commit(tok, reads, writes)
        if is_output:
            self.out_tokens.append(tok)
        return tok

    def finish(self):
        waits = self._waits("sp", self.out_tokens, True)
        nc = self.nc
        streams = self.stream

        def replay(name, e, extra=None):
            for (fn, ws, semh, inc) in streams[name]:
                for (key, val) in ws:
                    e.wait_ge(self._semh(key), val)
                fn(e).then_inc(semh, inc)
            if extra:
                for (key, val) in extra:
                    e.wait_ge(self._semh(key), val)

        with nc.Block() as block:
            @block.tensor
            def _(e):
                replay("pe", e)

            @block.scalar
            def _(e):
                replay("act", e)

            @block.vector
            def _(e):
                replay("dve", e)

            @block.gpsimd
            def _(e):
                replay("pool", e)

            @block.sync
            def _(e):
                replay("sp", e, waits)
        self.es.close()


def build_B(NT):
    nc = bass.Bass("TRN2", target_bir_lowering=False)
    oT = nc.dram_tensor("oT", [D_MODEL, NT], F32, kind="ExternalInput").ap()
    xres = nc.dram_tensor("xres", [NT, D_MODEL], F32, kind="ExternalInput").ap()
    w_out = nc.dram_tensor("w_out", [D_MODEL, D_MODEL], F32, kind="ExternalInput").ap()
    lnw = nc.dram_tensor("lnw", [D_MODEL], F32, kind="ExternalInput").ap()
    lnb = nc.dram_tensor("lnb", [D_MODEL], F32, kind="ExternalInput").ap()
    y = nc.dram_tensor("y", [NT, D_MODEL], F32, kind="ExternalOutput").ap()
    P = Prog(nc)
    emit_B(P, nc, oT, xres, w_out, lnw, lnb, y, NT)
    P.finish()
    return nc


def emit_B(P, nc, oT, xres, w_out, lnw, lnb, y, NT):
    KC = D_MODEL // 128
    w_sb = P.sb("w_sb", [128, KC, D_MODEL])
    lnw_sb = P.sb("lnw_sb", [128, D_MODEL])
    lnb_sb = P.sb("lnb_sb", [128, D_MODEL])
    NBUF = 2
    o_sb = [P.sb("o_sb%d" % i, [128, KC, 128]) for i in range(NBUF)]
    x_sb = [P.sb("x_sb%d" % i, [128, D_MODEL]) for i in range(NBUF)]
    z_sb = [P.sb("z_sb%d" % i, [128, D_MODEL]) for i in range(NBUF)]
    st_sb = [P.sb("st_sb%d" % i, [128, 2, nc.vector.BN_STATS_DIM]) for i in range(NBUF)]
    mv_sb = [P.sb("mv_sb%d" % i, [128, nc.vector.BN_AGGR_DIM]) for i in range(NBUF)]
    rs_sb = [P.sb("rs_sb%d" % i, [128, 1]) for i in range(NBUF)]
    pbank = [P.ps("pB%d" % i, [128, 512]) for i in range(4)]

    for kc in range(KC):
        P.dma("sp" if kc % 2 == 0 else "act", w_sb[:, kc, :], w_out[kc * 128:(kc + 1) * 128, :],
              writes=[("w", kc)])
    P.dma("sp", lnw_sb[:], lnw.partition_broadcast(128), writes=["lnw"])
    P.dma("sp", lnb_sb[:], lnb.partition_broadcast(128), writes=["lnb"])

    ntile = NT // 128
    oT_v = oT.rearrange("(kc p) t -> p kc t", p=128)
    for t in range(ntile):
        b = t % NBUF
        P.dma("sp", o_sb[b][:], oT_v[:, :, t * 128:(t + 1) * 128], writes=[("o", b)])
        P.dma("act", x_sb[b][:], xres[t * 128:(t + 1) * 128, :], writes=[("x", b)])
        for half in range(2):
            pb = (2 * t + half) % 4
            for kc in range(KC):
                P.op("pe", (lambda e, b=b, kc=kc, half=half, pb=pb: e.matmul(
                    pbank[pb][:], o_sb[b][:, kc, :], w_sb[:, kc, half * 512:(half + 1) * 512],
                    start=(kc == 0), stop=(kc == KC - 1))),
                    reads=[("o", b), ("w", kc)], writes=[("pB", pb)])
            P.op("dve", (lambda e, b=b, half=half, pb=pb: e.scalar_tensor_tensor(
                out=z_sb[b][:, half * 512:(half + 1) * 512], in0=x_sb[b][:, half * 512:(half + 1) * 512],
                scalar=float(DN_ALPHA), in1=pbank[pb][:], op0=ALU.mult, op1=ALU.add)),
                reads=[("x", b), ("pB", pb)], writes=[("z", b, half)])
            P.op("dve", (lambda e, b=b, half=half: e.bn_stats(
                out=st_sb[b][:, half, :], in_=z_sb[b][:, half * 512:(half + 1) * 512])),
                reads=[("z", b, half)], writes=[("st", b, half)])
        P.op("dve", (lambda e, b=b: e.bn_aggr(out=mv_sb[b][:], in_=st_sb[b][:])),
             reads=[("st", b, 0), ("st", b, 1)], writes=[("mv", b)])
        P.op("dve", (lambda e, b=b: e.tensor_scalar_add(
            out=rs_sb[b][:], in0=mv_sb[b][:, 1:2], scalar1=float(LN_EPS))),
            reads=[("mv", b)], writes=[("rs", b)])
        P.op("dve", (lambda e, b=b: e.reciprocal(out=rs_sb[b][:], in_=rs_sb[b][:])),
             reads=[("rs", b)], writes=[("rs", b)])
        P.op("act", (lambda e, b=b: e.activation(out=rs_sb[b][:], in_=rs_sb[b][:], func=AF.Sqrt)),
             reads=[("rs", b)], writes=[("rs", b)])
        P.op("dve", (lambda e, b=b: e.tensor_scalar(
            out=z_sb[b][:], in0=z_sb[b][:], scalar1=mv_sb[b][:, 0:1], scalar2=rs_sb[b][:, 0:1],
            op0=ALU.subtract, op1=ALU.mult)),
            reads=[("mv", b), ("rs", b), ("z", b, 0), ("z", b, 1)], writes=[("z", b, 0), ("z", b, 1)])
        P.op("pool", (lambda e, b=b: e.tensor_tensor(out=z_sb[b][:], in0=z_sb[b][:], in1=lnw_sb[:], op=ALU.mult)),
             reads=["lnw", ("z", b, 0), ("z", b, 1)], writes=[("z", b, 0), ("z", b, 1)])
        P.op("pool", (lambda e, b=b: e.tensor_tensor(out=z_sb[b][:], in0=z_sb[b][:], in1=lnb_sb[:], op=ALU.add)),
             reads=["lnb", ("z", b, 0), ("z", b, 1)], writes=[("z", b, 0), ("z", b, 1)])
        P.dma("sp", y[t * 128:(t + 1) * 128, :], z_sb[b][:], reads=[("z", b, 0), ("z", b, 1)],
              writes=[("yout", t)], is_output=True)


def run_B(o_full, x_full, w_out, lnw, lnb):
    NTOK = o_full.shape[0]
    NT = NTOK // NCORES
    nc = build_B(NT)
    in_maps = []
    for c in range(NCORES):
        sl = slice(c * NT, (c + 1) * NT)
        in_maps.append({
            "oT": np.ascontiguousarray(o_full[sl].T),
            "xres": np.ascontiguousarray(x_full[sl]),
            "w_out": np.ascontiguousarray(w_out),
            "lnw": np.ascontiguousarray(lnw),
            "lnb": np.ascontiguousarray(lnb),
        })
    res = run_bass_kernel_spmd(nc, in_maps, core_ids=list(range(NCORES)))
    return np.concatenate([r["y"] for r in res.results], axis=0)


TG = 512
CH = 64
NCH = TG // CH
KC = D_MODEL // 128


def _consts():
    c = {}
    c["ident"] = np.eye(128, dtype=np.float32)
    s = np.arange(64)[:, None]
    t = np.arange(64)[None, :]
    c["m_incl"] = (s <= t).astype(np.float32)
    m = np.ones((128, TG), np.float32)
    m[:, ::CH] = 0.0
    c["scanm"] = m
    strict = (s < t).astype(np.float32)
    incl = (s <= t).astype(np.float32)
    c["m_blk"] = np.block([[strict, incl], [strict, incl]]).astype(np.float32)
    c["m_low"] = (s > t).astype(np.float32)
    c["ones64"] = np.ones((64, 64), np.float32)
    return c


def load_x_group(P, xT, x_sb, g):
    xv = xT.rearrange("(kc p) t -> p kc t", p=128)
    for kc in range(KC):
        P.dma(("sp", "act")[kc % 2], x_sb[:, kc, :], xv[:, kc, g * TG:(g + 1) * TG],
              writes=[("x", kc)])


def proj_group(P, x_sb, w_sb, wkey, col0, M, pout, pkey):
    for kc in range(KC):
        P.op("pe", (lambda e, kc=kc: e.matmul(pout[0:M, :], w_sb[:, kc, col0:col0 + M], x_sb[:, kc, :],
                                              start=(kc == 0), stop=(kc == KC - 1))),
             reads=[("x", kc), wkey], writes=[pkey])


def build_gla(T):
    nc = bass.Bass("TRN2", target_bir_lowering=False)
    dt = lambda n, s: nc.dram_tensor(n, list(s), F32, kind="ExternalInput").ap()
    xT = dt("xT", [D_MODEL, T])
    wg = dt("wg", [D_MODEL, 272])
    a_up = dt("a_up", [16, 64])
    a_bias = dt("a_bias", [64, 1])
    norm_w = dt("norm_w", [64])
    cst = {k: dt("c_" + k, v.shape) for k, v in _consts().items()}
    og = nc.dram_tensor("og", [T, 64], F32, kind="ExternalOutput").ap()
    P = Prog(nc)
    emit_gla(P, nc, xT, wg, a_up, a_bias, norm_w, cst, og, T)
    P.finish()
    return nc


def emit_gla(P, nc, xT, wg, a_up, a_bias, norm_w, cst, og, T):
    NG = T // TG
    w_sb = P.sb("g_w", [128, KC, 272])
    x_sb = P.sb("g_x", [128, KC, TG])
    ident = P.sb("g_ident", [128, 128])
    m_incl = P.sb("g_mincl", [64, 64])
    scanm = P.sb("g_scanm", [64, TG])
    aup_sb = P.sb("g_aup", [16, 64])
    nbias = P.sb("g_nbias", [64, 1])
    nw_bc = P.sb("g_nw", [64, 64])
    qT = P.sb("g_qT", [64, TG])
    kT = P.sb("g_kT", [64, TG])
    vgT = P.sb("g_vgT", [128, TG])
    adn = P.sb("g_adn", [16, TG])
    l1 = P.sb("g_l1", [64, TG])
    cum = P.sb("g_cum", [64, TG])
    eg = P.sb("g_eg", [64, TG])
    eng = P.sb("g_eng", [64, TG])
    At = [P.sb("g_At%d" % i, [64, 64]) for i in range(2)]
    ktok = [P.sb("g_ktok%d" % i, [64, 64]) for i in range(2)]
    v_sb = [P.sb("g_v%d" % i, [64, 64]) for i in range(2)]
    sg_sb = [P.sb("g_sg%d" % i, [64, 64]) for i in range(2)]
    S = [P.sb("g_S%d" % i, [64, 64]) for i in range(2)]
    S0e = P.sb("g_S0e", [64, 64])
    sq = P.sb("g_sq", [64, 64])
    ss = [P.sb("g_ss%d" % i, [64, 1]) for i in range(2)]
    o_sb = [P.sb("g_o%d" % i, [64, NCH, 64]) for i in range(2)]
    pbig = [P.ps("g_pb%d" % i, [128, TG]) for i in range(2)]
    psm = [P.ps("g_ps%d" % i, [128, 512]) for i in range(6)]

    for kc in range(KC):
        P.dma(("sp", "act")[kc % 2], w_sb[:, kc, :], wg[kc * 128:(kc + 1) * 128, :], writes=["w"])
    P.dma("sp", ident[:], cst["ident"], writes=["ident"])
    P.dma("sp", m_incl[:], cst["m_incl"], writes=["m_incl"])
    P.dma("sp", scanm[:], cst["scanm"][0:64, :], writes=["scanm"])
    P.dma("sp", aup_sb[:], a_up, writes=["aup"])
    P.dma("sp", nbias[:], a_bias, writes=["nbias"])
    P.dma("sp", nw_bc[:], norm_w.partition_broadcast(64), writes=["nw"])
    P.op("dve", lambda e: e.tensor_scalar_mul(out=nbias[:], in0=nbias[:], scalar1=-1.0),
         reads=["nbias"], writes=["nbias"])
    P.op("dve", lambda e: e.memset(S[0][:], 0.0), writes=[("S", 0)])

    sm_i = [0]

    def small():
        i = sm_i[0]
        sm_i[0] = (i + 1) % 6
        return psm[i][:, 0:128], ("psm", i)

    nb = 0
    for g in range(NG):
        load_x_group(P, xT, x_sb, g)
        pb, pk = pbig[nb % 2], ("pb", nb % 2); nb += 1
        proj_group(P, x_sb, w_sb, "w", 0, 64, pb, pk)
        P.op("act", (lambda e, pb=pb: e.activation(out=qT[:], in_=pb[0:64, :], func=AF.Copy, scale=HD ** -0.5)),
             reads=[pk], writes=["qT"])
        pb, pk = pbig[nb % 2], ("pb", nb % 2); nb += 1
        proj_group(P, x_sb, w_sb, "w", 64, 64, pb, pk)
        P.op("dve", (lambda e, pb=pb: e.tensor_copy(out=kT[:], in_=pb[0:64, :])), reads=[pk], writes=["kT"])
        pb, pk = pbig[nb % 2], ("pb", nb % 2); nb += 1
        proj_group(P, x_sb, w_sb, "w", 128, 128, pb, pk)
        P.op("act", (lambda e, pb=pb: e.copy(out=vgT[:], in_=pb[:, :])), reads=[pk], writes=["vgT"])
        pb, pk = pbig[nb % 2], ("pb", nb % 2); nb += 1
        proj_group(P, x_sb, w_sb, "w", 256, 16, pb, pk)
        P.op("dve", (lambda e, pb=pb: e.tensor_copy(out=adn[:], in_=pb[0:16, :])), reads=[pk], writes=["adn"])
        pb, pk = pbig[nb % 2], ("pb", nb % 2); nb += 1
        P.op("pe", (lambda e, pb=pb: e.matmul(pb[0:64, :], aup_sb[:], adn[:], start=True, stop=True)),
             reads=["aup", "adn"], writes=[pk])
        P.op("act", (lambda e, pb=pb: e.activation(out=l1[:], in_=pb[0:64, :], func=AF.Exp, scale=-1.0,
                                                   bias=nbias[:, 0:1])), reads=[pk, "nbias"], writes=["l1"])
        P.op("act", (lambda e: e.activation(out=l1[:], in_=l1[:], func=AF.Ln, bias=1.0)),
             reads=["l1"], writes=["l1"])
        P.op("dve", (lambda e: e.tensor_tensor_scan(out=cum[:], data0=scanm[:], data1=l1[:], initial=0.0,
                                                    op0=ALU.mult, op1=ALU.add)),
             reads=["scanm", "l1"], writes=["cum"])
        P.op("act", (lambda e: e.activation(out=eg[:], in_=cum[:], func=AF.Exp, scale=-1.0 / 16.0)),
             reads=["cum"], writes=["eg"])
        P.op("act", (lambda e: e.activation(out=eng[:], in_=cum[:], func=AF.Exp, scale=1.0 / 16.0)),
             reads=["cum"], writes=["eng"])
        P.op("dve", (lambda e: e.tensor_tensor(out=qT[:], in0=qT[:], in1=eg[:], op=ALU.mult)),
             reads=["qT", "eg"], writes=["qT"])
        P.op("pool", (lambda e: e.tensor_tensor(out=kT[:], in0=kT[:], in1=eng[:], op=ALU.mult)),
             reads=["kT", "eng"], writes=["kT"])
        ob = g % 2
        for c in range(NCH):
            ci = g * NCH + c
            b2 = ci % 2
            cs = slice(c * CH, (c + 1) * CH)
            pa, pak = small()
            P.op("pe", (lambda e, pa=pa, cs=cs: e.matmul(pa[0:64, 0:64], kT[:, cs], qT[:, cs], start=True, stop=True)),
                 reads=["kT", "qT"], writes=[pak])
            P.op("dve", (lambda e, pa=pa, b2=b2: e.tensor_tensor(out=At[b2][:], in0=pa[0:64, 0:64], in1=m_incl[:],
                                                                 op=ALU.mult)),
                 reads=[pak, "m_incl"], writes=[("At", b2)])
            pt, ptk = small()
            P.op("pe", (lambda e, pt=pt, cs=cs: e.transpose(pt[0:64, 0:64], kT[:, cs], ident[0:64, 0:64])),
                 reads=["kT", "ident"], writes=[ptk])
            P.op("act", (lambda e, pt=pt, b2=b2: e.copy(out=ktok[b2][:], in_=pt[0:64, 0:64])),
                 reads=[ptk], writes=[("ktok", b2)])
            pv, pvk = small()
            P.op("pe", (lambda e, pv=pv, cs=cs: e.transpose(pv[0:64, :], vgT[:, cs], ident[:, :])),
                 reads=["vgT", "ident"], writes=[pvk])
            P.op("dve", (lambda e, pv=pv, b2=b2: e.tensor_copy(out=v_sb[b2][:], in_=pv[0:64, 0:64])),
                 reads=[pvk], writes=[("v", b2)])
            P.op("act", (lambda e, pv=pv, b2=b2: e.activation(out=sg_sb[b2][:], in_=pv[0:64, 64:128], func=AF.Silu)),
                 reads=[pvk], writes=[("sg", b2)])
            P.op("pool", (lambda e, b2=b2: e.tensor_tensor(out=sg_sb[b2][:], in0=sg_sb[b2][:], in1=nw_bc[:],
                                                           op=ALU.mult)),
                 reads=[("sg", b2), "nw"], writes=[("sg", b2)])
            pkv, pkvk = small()
            P.op("pe", (lambda e, pkv=pkv, b2=b2: e.matmul(pkv[0:64, 0:64], ktok[b2][:], v_sb[b2][:],
                                                           start=True, stop=True)),
                 reads=[("ktok", b2), ("v", b2)], writes=[pkvk])
            py, pyk = small()
            s_old, s_new = ci % 2, (ci + 1) % 2
            P.op("pe", (lambda e, py=py, cs=cs, s_old=s_old: e.matmul(py[0:64, 0:64], qT[:, cs], S[s_old][:],
                                                                      start=True, stop=False)),
                 reads=["qT", ("S", s_old)], writes=[pyk])
            P.op("pe", (lambda e, py=py, b2=b2: e.matmul(py[0:64, 0:64], At[b2][:], v_sb[b2][:],
                                                         start=False, stop=True)),
                 reads=[("At", b2), ("v", b2)], writes=[pyk])
            el = eg[:, c * CH + CH - 1:c * CH + CH]
            P.op("dve", (lambda e, s_old=s_old, el=el: e.tensor_scalar_mul(out=S0e[:], in0=S[s_old][:], scalar1=el)),
                 reads=[("S", s_old), "eg"], writes=["S0e"])
            P.op("dve", (lambda e, pkv=pkv, s_new=s_new, el=el: e.scalar_tensor_tensor(
                out=S[s_new][:], in0=pkv[0:64, 0:64], scalar=el, in1=S0e[:], op0=ALU.mult, op1=ALU.add)),
                reads=[pkvk, "S0e", "eg"], writes=[("S", s_new)])
            P.op("act", (lambda e, py=py: e.activation(out=sq[:], in_=py[0:64, 0:64], func=AF.Square)),
                 reads=[pyk], writes=["sq"])
            P.op("dve", (lambda e, b2=b2: e.reduce_sum(out=ss[b2][:], in_=sq[:], axis=AX.X)),
                 reads=["sq"], writes=[("ss", b2)])
            P.op("dve", (lambda e, b2=b2: e.tensor_scalar(out=ss[b2][:], in0=ss[b2][:], scalar1=1.0 / 64.0,
                                                          scalar2=float(LN_EPS), op0=ALU.mult, op1=ALU.add)),
                 reads=[("ss", b2)], writes=[("ss", b2)])
            P.op("dve", (lambda e, b2=b2: e.reciprocal(out=ss[b2][:], in_=ss[b2][:])),
                 reads=[("ss", b2)], writes=[("ss", b2)])
            P.op("act", (lambda e, b2=b2: e.activation(out=ss[b2][:], in_=ss[b2][:], func=AF.Sqrt)),
                 reads=[("ss", b2)], writes=[("ss", b2)])
            P.op("dve", (lambda e, py=py, b2=b2, ob=ob, c=c: e.scalar_tensor_tensor(
                out=o_sb[ob][:, c, :], in0=py[0:64, 0:64], scalar=ss[b2][:, 0:1], in1=sg_sb[b2][:],
                op0=ALU.mult, op1=ALU.mult)),
                reads=[pyk, ("ss", b2), ("sg", b2)], writes=[("o", ob)])
        P.dma("sp", og[g * TG:(g + 1) * TG, :].rearrange("(c t) v -> t c v", t=CH), o_sb[ob][:],
              reads=[("o", ob)], writes=[("og", g)], is_output=True)


def run_gla(xT_b, w_in_l, a_up_l, a_bias_l, norm_w_l, T=SEQ):
    nc = build_gla(T)
    cst = _consts()
    in_maps = []
    for c in range(NCORES):
        b, j = c // 4, c % 4
        cols = np.concatenate([np.arange(256 * i + 64 * j, 256 * i + 64 * j + 64) for i in range(4)]
                              + [np.arange(1024, 1040)])
        m = {"xT": xT_b[b], "wg": np.ascontiguousarray(w_in_l[:, cols]),
             "a_up": np.ascontiguousarray(a_up_l[:, 64 * j:64 * j + 64]),
             "a_bias": np.ascontiguousarray(a_bias_l[64 * j:64 * j + 64].reshape(64, 1)),
             "norm_w": np.ascontiguousarray(norm_w_l)}
        for k, v in cst.items():
            m["c_" + k] = v
        in_maps.append(m)
    res = run_bass_kernel_spmd(nc, in_maps, core_ids=list(range(NCORES)))
    out = np.zeros((BATCH, T, 256), np.float32)
    for c in range(NCORES):
        b, j = c // 4, c % 4
        out[b, :, 64 * j:64 * j + 64] = res.results[c]["og"]
    return out


C0 = float(np.exp(-0.5))
GN_EPS = 64e-5
DBG = {"stop": None}


def build_rwkv(T):
    nc = bass.Bass("TRN2", target_bir_lowering=False)
    dt = lambda n, s: nc.dram_tensor(n, list(s), F32, kind="ExternalInput").ap()
    ins = dict(xT=dt("xT", [D_MODEL, T]), wr=dt("wr", [D_MODEL, 320]), mu=dt("mu", [320, 1]),
               w_up=dt("w_up", [32, 64]), a_up=dt("a_up", [32, 64]))
    for n in ("w0", "a0", "k_k", "k_a", "r_k"):
        ins[n] = dt(n, [64, 1])
    for n in ("gn_w", "gn_b"):
        ins[n] = dt(n, [64])
    cst = {k: dt("c_" + k, v.shape) for k, v in _consts().items()}
    orw = nc.dram_tensor("orw", [T, 64], F32, kind="ExternalOutput").ap()
    P = Prog(nc)
    emit_rwkv(P, nc, ins, cst, orw, T)
    P.finish()
    return nc


def emit_rwkv(P, nc, ins, cst, orw, T):
    NG = T // TG
    xT = ins["xT"]
    w_sb = P.sb("r_w", [128, KC, 320])
    x_sb = P.sb("r_x", [128, KC, TG])
    ident = P.sb("r_ident", [128, 128])
    m_blk = P.sb("r_mblk", [128, 128])
    m_low = P.sb("r_mlow", [64, 64])
    ones = P.sb("r_ones", [64, 64])
    scanm = P.sb("r_scanm", [64, TG])
    mu_r = P.sb("r_mu_r", [64, 2]); mu_k = P.sb("r_mu_k", [64, 2])
    mu_vg = P.sb("r_mu_vg", [128, 2]); mu_wa = P.sb("r_mu_wa", [64, 2])
    wup = P.sb("r_wup", [32, 64])
    aup = P.sb("r_aup", [64, 64])
    pv = {n: P.sb("r_" + n, [64, 1]) for n in ("w0", "a0", "k_k", "k_a", "r_k")}
    omka = P.sb("r_omka", [64, 1])
    gnw = P.sb("r_gnw", [64, 64]); gnb = P.sb("r_gnb", [64, 64])
    H_r = P.sb("r_Hr", [64, TG + 1]); H_k = P.sb("r_Hk", [64, TG + 1])
    H_vg = P.sb("r_Hvg", [128, TG + 1]); H_wa = P.sb("r_Hwa", [64, TG + 1])
    tmp = P.sb("r_tmp", [128, TG])
    pr = P.sb("r_pr", [64, TG]); pk = P.sb("r_pk", [64, TG])
    pvg = P.sb("r_pvg", [128, CH + TG]); pwa = P.sb("r_pwa", [64, TG])
    tw = P.sb("r_tw", [32, TG]); sgw = P.sb("r_sgw", [64, TG]); aT = P.sb("r_aT", [64, TG])
    cum = P.sb("r_cum", [64, TG]); cx = P.sb("r_cx", [64, TG])
    eg = P.sb("r_eg", [64, TG]); eng = P.sb("r_eng", [64, TG]); egx = P.sb("r_egx", [64, TG])
    kk = P.sb("r_kk", [64, TG]); kk2 = P.sb("r_kk2", [64, TG]); nrm = P.sb("r_nrm", [64, TG])
    fac = P.sb("r_fac", [64, TG]); kp = P.sb("r_kp", [64, TG]); bb = P.sb("r_bb", [64, TG])
    rk = P.sb("r_rk", [64, TG])
    AR = P.sb("r_AR", [64, NCH, 2, CH]); BK = P.sb("r_BK", [64, NCH, 2, CH])
    Am = [P.sb("r_Am%d" % i, [128, 128]) for i in range(2)]
    Aak = [P.sb("r_Aak%d" % i, [64, 64]) for i in range(2)]
    Pm = [P.sb("r_P%d" % i, [64, 64]) for i in range(2)]
    Qm = [P.sb("r_Q%d" % i, [64, 64]) for i in range(2)]
    Wm = [P.sb("r_W%d" % i, [64, 64]) for i in range(2)]
    BKtok = [P.sb("r_BKtok%d" % i, [128, 64]) for i in range(2)]
    UV = [P.sb("r_UV%d" % i, [128, 64]) for i in range(2)]
    vtok = [P.sb("r_vtok%d" % i, [64, 64]) for i in range(2)]
    sg = [P.sb("r_sg%d" % i, [64, 64]) for i in range(2)]
    X_sb = P.sb("r_X", [64, 64])
    S = [P.sb("r_S%d" % i, [64, 64]) for i in range(2)]
    S0e = P.sb("r_S0e", [64, 64])
    st = P.sb("r_st", [64, nc.vector.BN_STATS_DIM]); mv = P.sb("r_mv", [64, nc.vector.BN_AGGR_DIM])
    rs = P.sb("r_rs", [64, 1]); bon = P.sb("r_bon", [64, 1])
    yn = P.sb("r_yn", [64, 64])
    o_sb = [P.sb("r_o%d" % i, [64, NCH, 64]) for i in range(2)]
    pbig = [P.ps("r_pb%d" % i, [128, TG]) for i in range(2)]
    psm = [P.ps("r_ps%d" % i, [128, 512]) for i in range(6)]

    for kc in range(KC):
        P.dma(("sp", "act")[kc % 2], w_sb[:, kc, :], ins["wr"][kc * 128:(kc + 1) * 128, :], writes=["w"])
    P.dma("sp", ident[:], cst["ident"], writes=["ident"])
    P.dma("sp", m_blk[:], cst["m_blk"], writes=["m_blk"])
    P.dma("sp", m_low[:], cst["m_low"], writes=["m_low"])
    P.dma("sp", ones[:], cst["ones64"], writes=["ones"])
    P.dma("sp", scanm[:], cst["scanm"][0:64, :], writes=["scanm"])
    mu = ins["mu"]
    P.dma("sp", mu_r[:, 0:1], mu[0:64, :], writes=["mu_r"])
    P.dma("sp", mu_k[:, 0:1], mu[64:128, :], writes=["mu_k"])
    P.dma("sp", mu_vg[:, 0:1], mu[128:256, :], writes=["mu_vg"])
    P.dma("sp", mu_wa[:, 0:1], mu[256:320, :], writes=["mu_wa"])
    for (m_, k_) in ((mu_r, "mu_r"), (mu_k, "mu_k"), (mu_vg, "mu_vg"), (mu_wa, "mu_wa")):
        P.op("dve", (lambda e, m_=m_: e.tensor_scalar(out=m_[:, 1:2], in0=m_[:, 0:1], scalar1=-1.0, scalar2=1.0,
                                                      op0=ALU.mult, op1=ALU.add)), reads=[k_], writes=[k_])
    P.dma("sp", wup[:], ins["w_up"], writes=["wup"])
    P.dma("sp", aup[32:64, :], ins["a_up"], writes=["aup"])
    for n in pv:
        P.dma("sp", pv[n][:], ins[n], writes=[n])
    P.op("dve", (lambda e: e.tensor_scalar(out=omka[:], in0=pv["k_a"][:], scalar1=-1.0, scalar2=1.0,
                                           op0=ALU.mult, op1=ALU.add)), reads=["k_a"], writes=["omka"])
    P.dma("sp", gnw[:], ins["gn_w"].partition_broadcast(64), writes=["gnw"])
    P.dma("sp", gnb[:], ins["gn_b"].partition_broadcast(64), writes=["gnb"])
    P.op("dve", lambda e: e.memset(S[0][:], 0.0), writes=[("S", 0)])
    for (H_, k_) in ((H_r, "H_r"), (H_k, "H_k"), (H_vg, "H_vg"), (H_wa, "H_wa")):
        P.op("pool", (lambda e, H_=H_: e.memset(H_[:, 0:1], 0.0)), writes=[k_])
    P.op("pool", lambda e: e.memset(pvg[:, 0:CH], 0.0), writes=["pvg"])

    sm_i = [0]

    def small():
        i = sm_i[0]
        sm_i[0] = (i + 1) % 6
        return psm[i][:, 0:128], ("psm", i)

    nb = [0]

    def big():
        i = nb[0] % 2
        nb[0] += 1
        return pbig[i], ("pb", i)

    def shift_mix(pb, pkey, M, H_, hk, mu_, mk, dst, dk, ev):
        if ev == "act":
            P.op("act", (lambda e: e.copy(out=H_[:, 1:TG + 1], in_=pb[0:M, :])), reads=[pkey], writes=[hk])
        else:
            P.op("dve", (lambda e: e.tensor_copy(out=H_[:, 1:TG + 1], in_=pb[0:M, :])), reads=[pkey], writes=[hk])
        P.op("pool", (lambda e: e.tensor_scalar(out=tmp[0:M, :], in0=H_[:, 0:TG], scalar1=mu_[:, 0:1], scalar2=None,
                                                op0=ALU.mult)), reads=[hk, mk], writes=["tmp"])
        P.op("dve", (lambda e: e.scalar_tensor_tensor(out=dst, in0=H_[:, 1:TG + 1], scalar=mu_[:, 1:2],
                                                      in1=tmp[0:M, :], op0=ALU.mult, op1=ALU.add)),
             reads=[hk, mk, "tmp"], writes=[dk])
        P.op("pool", (lambda e: e.tensor_copy(out=H_[:, 0:1], in_=H_[:, TG:TG + 1])), reads=[hk], writes=[hk])

    v3 = lambda t: t[:].rearrange("p (c t) -> p c t", t=CH)

    for g in range(NG):
        load_x_group(P, xT, x_sb, g)
        pb, pkk = big(); proj_group(P, x_sb, w_sb, "w", 0, 64, pb, pkk)
        shift_mix(pb, pkk, 64, H_r, "H_r", mu_r, "mu_r", pr[:], "pr", "act")
        pb, pkk = big(); proj_group(P, x_sb, w_sb, "w", 64, 64, pb, pkk)
        shift_mix(pb, pkk, 64, H_k, "H_k", mu_k, "mu_k", pk[:], "pk", "dve")
        pb, pkk = big(); proj_group(P, x_sb, w_sb, "w", 128, 128, pb, pkk)
        shift_mix(pb, pkk, 128, H_vg, "H_vg", mu_vg, "mu_vg", pvg[:, CH:CH + TG], "pvg", "act")
        pb, pkk = big(); proj_group(P, x_sb, w_sb, "w", 256, 64, pb, pkk)
        shift_mix(pb, pkk, 64, H_wa, "H_wa", mu_wa, "mu_wa", pwa[:], "pwa", "dve")
        P.op("act", lambda e: e.activation(out=tw[:], in_=pwa[0:32, :], func=AF.Tanh), reads=["pwa"], writes=["tw"])
        pb, pkk = big()
        P.op("pe", (lambda e, pb=pb: e.matmul(pb[0:64, :], wup[:], tw[:], start=True, stop=True)),
             reads=["wup", "tw"], writes=[pkk])
        P.op("act", (lambda e, pb=pb: e.activation(out=sgw[:], in_=pb[0:64, :], func=AF.Sigmoid, bias=pv["w0"][:, 0:1])),
             reads=[pkk, "w0"], writes=["sgw"])
        pb, pkk = big()
        P.op("pe", (lambda e, pb=pb: e.matmul(pb[0:64, :], aup[32:64, :], pwa[32:64, :], start=True, stop=True)),
             reads=["aup", "pwa"], writes=[pkk])
        P.op("act", (lambda e, pb=pb: e.activation(out=aT[:], in_=pb[0:64, :], func=AF.Sigmoid, bias=pv["a0"][:, 0:1])),
             reads=[pkk, "a0"], writes=["aT"])
        P.op("dve", lambda e: e.tensor_tensor_scan(out=cum[:], data0=scanm[:], data1=sgw[:], initial=0.0,
                                                   op0=ALU.mult, op1=ALU.add), reads=["scanm", "sgw"], writes=["cum"])
        P.op("pool", lambda e: e.tensor_tensor(out=cx[:], in0=cum[:], in1=sgw[:], op=ALU.subtract),
             reads=["cum", "sgw"], writes=["cx"])
        P.op("act", lambda e: e.activation(out=eg[:], in_=cum[:], func=AF.Exp, scale=-C0), reads=["cum"], writes=["eg"])
        P.op("act", lambda e: e.activation(out=eng[:], in_=cum[:], func=AF.Exp, scale=C0), reads=["cum"], writes=["eng"])
        P.op("act", lambda e: e.activation(out=egx[:], in_=cx[:], func=AF.Exp, scale=-C0), reads=["cx"], writes=["egx"])
        P.op("dve", lambda e: e.tensor_scalar_mul(out=kk[:], in0=pk[:], scalar1=pv["k_k"][:, 0:1]),
             reads=["pk", "k_k"], writes=["kk"])
        P.op("pool", lambda e: e.tensor_tensor(out=kk2[:], in0=kk[:], in1=kk[:], op=ALU.mult), reads=["kk"], writes=["kk2"])
        pb, pkk = big()
        P.op("pe", (lambda e, pb=pb: e.matmul(pb[0:64, :], ones[:], kk2[:], start=True, stop=True)),
             reads=["ones", "kk2"], writes=[pkk])
        P.op("act", (lambda e, pb=pb: e.activation(out=nrm[:], in_=pb[0:64, :], func=AF.Sqrt)), reads=[pkk], writes=["nrm"])
        P.op("dve", lambda e: e.tensor_scalar_max(out=nrm[:], in0=nrm[:], scalar1=1e-12), reads=["nrm"], writes=["nrm"])
        P.op("dve", lambda e: e.reciprocal(out=nrm[:], in_=nrm[:]), reads=["nrm"], writes=["nrm"])
        P.op("dve", lambda e: e.tensor_tensor(out=kk[:], in0=kk[:], in1=nrm[:], op=ALU.mult), reads=["kk", "nrm"], writes=["kk"])
        P.op("dve", lambda e: e.tensor_scalar(out=fac[:], in0=aT[:], scalar1=pv["k_a"][:, 0:1], scalar2=omka[:, 0:1],
                                              op0=ALU.mult, op1=ALU.add), reads=["aT", "k_a", "omka"], writes=["fac"])
        P.op("pool", lambda e: e.tensor_tensor(out=kp[:], in0=pk[:], in1=fac[:], op=ALU.mult), reads=["pk", "fac"], writes=["kp"])
        P.op("pool", lambda e: e.tensor_tensor(out=bb[:], in0=kk[:], in1=aT[:], op=ALU.mult), reads=["kk", "aT"], writes=["bb"])
        P.op("dve", lambda e: e.scalar_tensor_tensor(out=rk[:], in0=pr[:], scalar=pv["r_k"][:, 0:1], in1=kp[:],
                                                     op0=ALU.mult, op1=ALU.mult), reads=["pr", "r_k", "kp"], writes=["rk"])
        P.op("dve", lambda e: e.scalar_tensor_tensor(out=AR[:, :, 0, :], in0=v3(kk), scalar=-1.0, in1=v3(egx),
                                                     op0=ALU.mult, op1=ALU.mult), reads=["kk", "egx"], writes=["AR"])
        P.op("pool", lambda e: e.tensor_tensor(out=AR[:, :, 1, :], in0=v3(pr), in1=v3(eg), op=ALU.mult),
             reads=["pr", "eg"], writes=["AR"])
        P.op("dve", lambda e: e.tensor_tensor(out=BK[:, :, 0, :], in0=v3(bb), in1=v3(eng), op=ALU.mult),
             reads=["bb", "eng"], writes=["BK"])
        P.op("pool", lambda e: e.tensor_tensor(out=BK[:, :, 1, :], in0=v3(kp), in1=v3(eng), op=ALU.mult),
             reads=["kp", "eng"], writes=["BK"])
        ob = g % 2
        if DBG["stop"]:
            P.op("dve", (lambda e, ob=ob: e.memset(o_sb[ob][:], 0.0)), writes=[("o", ob)])
        for c in range(NCH):
            if DBG["stop"] == "prep":
                break
            ci = g * NCH + c
            b2 = ci % 2
            ARc = AR[:, c, :, :].rearrange("p a t -> p (a t)")
            BKc = BK[:, c, :, :].rearrange("p a t -> p (a t)")
            pa, pak = small()
            P.op("pe", (lambda e, pa=pa, ARc=ARc, BKc=BKc: e.matmul(pa[:, :], BKc, ARc, start=True, stop=True)),
                 reads=["AR", "BK"], writes=[pak])
            P.op("dve", (lambda e, pa=pa, b2=b2: e.tensor_tensor(out=Am[b2][:], in0=pa[:, :], in1=m_blk[:], op=ALU.mult)),
                 reads=[pak, "m_blk"], writes=[("Am", b2)])
            pk2, pk2k = small()
            P.op("pe", (lambda e, pk2=pk2, c=c: e.matmul(pk2[0:64, 0:64], BK[:, c, 1, :], AR[:, c, 0, :], start=True, stop=True)),
                 reads=["AR", "BK"], writes=[pk2k])
            P.op("dve", (lambda e, pk2=pk2, b2=b2: e.tensor_tensor(out=Aak[b2][:], in0=pk2[0:64, 0:64], in1=m_blk[0:64, 0:64], op=ALU.mult)),
                 reads=[pk2k, "m_blk"], writes=[("Aak", b2)])
            pn, pnk = small()
            P.op("pe", (lambda e, pn=pn, c=c: e.matmul(pn[0:64, 0:64], AR[:, c, 0, :], BK[:, c, 0, :], start=True, stop=True)),
                 reads=["AR", "BK"], writes=[pnk])
            P.op("dve", (lambda e, pn=pn: e.tensor_tensor(out=Pm[0][:], in0=pn[0:64, 0:64], in1=m_low[:], op=ALU.mult)),
                 reads=[pnk, "m_low"], writes=[("P", 0)])
            if DBG["stop"] == "scores":
                continue
            P.op("act", (lambda e, b2=b2: e.copy(out=Qm[0][:], in_=Am[b2][0:64, 0:64])), reads=[("Am", b2)], writes=[("Q", 0)])
            P.op("pool", (lambda e: e.tensor_tensor(out=Wm[0][:], in0=Qm[0][:], in1=ident[0:64, 0:64], op=ALU.add)),
                 reads=[("Q", 0), "ident"], writes=[("W", 0)])
            for i in range(1, 6):
                a_, b_ = (i - 1) % 2, i % 2
                pp, ppk = small()
                P.op("pe", (lambda e, pp=pp, a_=a_: e.matmul(pp[0:64, 0:64], Qm[a_][:], Pm[a_][:], start=True, stop=True)),
                     reads=[("Q", a_), ("P", a_)], writes=[ppk])
                if i <= 4:
                    pq, pqk = small()
                    P.op("pe", (lambda e, pq=pq, a_=a_: e.matmul(pq[0:64, 0:64], Pm[a_][:], Qm[a_][:], start=True, stop=True)),
                         reads=[("Q", a_), ("P", a_)], writes=[pqk])
                P.op("act", (lambda e, pp=pp, b_=b_: e.copy(out=Pm[b_][:], in_=pp[0:64, 0:64])), reads=[ppk], writes=[("P", b_)])
                if i <= 4:
                    P.op("dve", (lambda e, pq=pq, b_=b_: e.tensor_copy(out=Qm[b_][:], in_=pq[0:64, 0:64])),
                         reads=[pqk], writes=[("Q", b_)])
                pw, pwk = small()
                P.op("pe", (lambda e, pw=pw, a_=a_, b_=b_: e.matmul(pw[0:64, 0:64], Pm[b_][:], Wm[a_][:], start=True, stop=True)),
                     reads=[("P", b_), ("W", a_)], writes=[pwk])
                P.op("dve", (lambda e, pw=pw, a_=a_, b_=b_: e.tensor_tensor(out=Wm[b_][:], in0=pw[0:64, 0:64], in1=Wm[a_][:],
                                                                            op=ALU.add)),
                     reads=[pwk, ("W", a_)], writes=[("W", b_)])
            Wf = Wm[5 % 2]
            wfk = ("W", 5 % 2)
            if DBG["stop"] == "doubling":
                continue
            pt, ptk = small()
            P.op("pe", (lambda e, pt=pt, BKc=BKc: e.transpose(pt[:, 0:64], BKc, ident[0:64, 0:64])),
                 reads=["BK", "ident"], writes=[ptk])
            P.op("act", (lambda e, pt=pt, b2=b2: e.copy(out=BKtok[b2][:], in_=pt[:, 0:64])), reads=[ptk], writes=[("BKtok", b2)])
            if DBG["stop"] == "t1":
                continue
            pt, ptk = small()
            P.op("pe", (lambda e, pt=pt, c=c: e.transpose(pt[:, 0:64], pvg[0:64, c * CH:c * CH + 128], ident[0:64, 0:64])),
                 reads=["pvg", "ident"], writes=[ptk])
            P.op("dve", (lambda e, pt=pt, b2=b2: e.tensor_copy(out=UV[b2][64:128, :], in_=pt[64:128, 0:64])),
                 reads=[ptk], writes=[("UVv", b2)])
            if DBG["stop"] == "t2":
                continue
            pt, ptk = small()
            P.op("pe", (lambda e, pt=pt, c=c: e.transpose(pt[0:64, :], pvg[:, CH + c * CH:CH + (c + 1) * CH], ident[:, :])),
                 reads=["pvg", "ident"], writes=[ptk])
            P.op("dve", (lambda e, pt=pt, b2=b2: e.tensor_copy(out=vtok[b2][:], in_=pt[0:64, 0:64])), reads=[ptk], writes=[("vtok", b2)])
            if DBG["stop"] == "t3":
                continue
            P.op("act", (lambda e, pt=pt, b2=b2: e.activation(out=sg[b2][:], in_=pt[0:64, 64:128], func=AF.Sigmoid)),
                 reads=[ptk], writes=[("sg", b2)])
            P.op("dve", (lambda e, pt=pt, b2=b2: e.tensor_tensor(out=sg[b2][:], in0=pt[0:64, 64:128], in1=sg[b2][:], op=ALU.mult)),
                 reads=[ptk, ("sg", b2)], writes=[("sg", b2)])
            if DBG["stop"] == "transposes":
                continue
            s_old, s_new = ci % 2, (ci + 1) % 2
            px, pxk = small()
            P.op("pe", (lambda e, px=px, c=c, s_old=s_old: e.matmul(px[0:64, 0:64], AR[:, c, 0, :], S[s_old][:], start=True, stop=False)),
                 reads=["AR", ("S", s_old)], writes=[pxk])
            P.op("pe", (lambda e, px=px, b2=b2: e.matmul(px[0:64, 0:64], Aak[b2][:], vtok[b2][:], start=False, stop=True)),
                 reads=[("Aak", b2), ("vtok", b2)], writes=[pxk])
            P.op("act", (lambda e, px=px: e.copy(out=X_sb[:], in_=px[0:64, 0:64])), reads=[pxk], writes=["X"])
            pu, puk = small()
            P.op("pe", (lambda e, pu=pu, Wf=Wf: e.matmul(pu[0:64, 0:64], Wf[:], X_sb[:], start=True, stop=True)),
                 reads=[wfk, "X"], writes=[puk])
            P.op("dve", (lambda e, pu=pu, b2=b2: e.tensor_copy(out=UV[b2][0:64, :], in_=pu[0:64, 0:64])), reads=[puk], writes=[("UVu", b2)])
            pT, pTk = small()
            P.op("pe", (lambda e, pT=pT, b2=b2: e.matmul(pT[0:64, 0:64], BKtok[b2][:], UV[b2][:], start=True, stop=True)),
                 reads=[("BKtok", b2), ("UVu", b2), ("UVv", b2)], writes=[pTk])
            py, pyk = small()
            P.op("pe", (lambda e, py=py, c=c, s_old=s_old: e.matmul(py[0:64, 0:64], AR[:, c, 1, :], S[s_old][:], start=True, stop=False)),
                 reads=["AR", ("S", s_old)], writes=[pyk])
            P.op("pe", (lambda e, py=py, b2=b2: e.matmul(py[0:64, 0:64], Am[b2][:, 64:128], UV[b2][:], start=False, stop=True)),
                 reads=[("Am", b2), ("UVu", b2), ("UVv", b2)], writes=[pyk])
            el = eg[:, c * CH + CH - 1:c * CH + CH]
            P.op("dve", (lambda e, s_old=s_old, el=el: e.tensor_scalar_mul(out=S0e[:], in0=S[s_old][:], scalar1=el)),
                 reads=[("S", s_old), "eg"], writes=["S0e"])
            P.op("dve", (lambda e, pT=pT, s_new=s_new, el=el: e.scalar_tensor_tensor(
                out=S[s_new][:], in0=pT[0:64, 0:64], scalar=el, in1=S0e[:], op0=ALU.mult, op1=ALU.add)),
                reads=[pTk, "S0e", "eg"], writes=[("S", s_new)])
            if DBG["stop"] == "serial":
                continue
            pbn, pbnk = small()
            P.op("pe", (lambda e, pbn=pbn, c=c: e.matmul(pbn[0:64, 0:1], rk[:, c * CH:(c + 1) * CH], ones[:, 0:1], start=True, stop=True)),
                 reads=["rk", "ones"], writes=[pbnk])
            P.op("act", (lambda e, pbn=pbn: e.copy(out=bon[:], in_=pbn[0:64, 0:1])), reads=[pbnk], writes=["bon"])
            P.op("dve", (lambda e, py=py: e.bn_stats(out=st[:], in_=py[0:64, 0:64])), reads=[pyk], writes=["st"])
            P.op("dve", (lambda e: e.bn_aggr(out=mv[:], in_=st[:])), reads=["st"], writes=["mv"])
            P.op("dve", (lambda e: e.tensor_scalar_add(out=rs[:], in0=mv[:, 1:2], scalar1=float(GN_EPS))), reads=["mv"], writes=["rs"])
            P.op("dve", (lambda e: e.reciprocal(out=rs[:], in_=rs[:])), reads=["rs"], writes=["rs"])
            P.op("act", (lambda e: e.activation(out=rs[:], in_=rs[:], func=AF.Sqrt)), reads=["rs"], writes=["rs"])
            P.op("dve", (lambda e, py=py: e.tensor_scalar(out=yn[:], in0=py[0:64, 0:64], scalar1=mv[:, 0:1], scalar2=rs[:, 0:1],
                                                          op0=ALU.subtract, op1=ALU.mult)),
                 reads=[pyk, "mv", "rs"], writes=["yn"])
            P.op("pool", (lambda e: e.tensor_tensor(out=yn[:], in0=yn[:], in1=gnw[:], op=ALU.mult)), reads=["yn", "gnw"], writes=["yn"])
            P.op("pool", (lambda e: e.tensor_tensor(out=yn[:], in0=yn[:], in1=gnb[:], op=ALU.add)), reads=["yn", "gnb"], writes=["yn"])
            P.op("dve", (lambda e, b2=b2: e.scalar_tensor_tensor(out=yn[:], in0=vtok[b2][:], scalar=bon[:, 0:1], in1=yn[:],
                                                                 op0=ALU.mult, op1=ALU.add)),
                 reads=[("vtok", b2), "bon", "yn"], writes=["yn"])
            P.op("dve", (lambda e, b2=b2, ob=ob, c=c: e.tensor_tensor(out=o_sb[ob][:, c, :], in0=yn[:], in1=sg[b2][:], op=ALU.mult)),
                 reads=["yn", ("sg", b2)], writes=[("o", ob)])
        P.dma("sp", orw[g * TG:(g + 1) * TG, :].rearrange("(c t) v -> t c v", t=CH), o_sb[ob][:],
              reads=[("o", ob)], writes=[("oo", g)], is_output=True)


def rwkv_inmap(b, j, xT_b, w_in_l, prm):
    base = 1040 + 2048
    rel = np.concatenate([np.arange(256 * i + 64 * j, 256 * i + 64 * j + 64) for i in range(4)]
                         + [np.arange(1024, 1088)])
    hs = slice(64 * j, 64 * j + 64)
    col = lambda v: np.ascontiguousarray(v[hs].reshape(64, 1))
    m = {"xT": xT_b[b], "wr": np.ascontiguousarray(w_in_l[:, base + rel]),
         "mu": np.ascontiguousarray(prm["rwkv_mu"][rel].reshape(320, 1)),
         "w_up": np.ascontiguousarray(prm["rwkv_w_up"][:, hs]), "a_up": np.ascontiguousarray(prm["rwkv_a_up"][:, hs]),
         "w0": col(prm["rwkv_w0"]), "a0": col(prm["rwkv_a0"]), "k_k": col(prm["rwkv_k_k"]),
         "k_a": col(prm["rwkv_k_a"]), "r_k": col(prm["rwkv_r_k"]),
         "gn_w": np.ascontiguousarray(prm["rwkv_gn_w"][hs]), "gn_b": np.ascontiguousarray(prm["rwkv_gn_b"][hs])}
    for k, v in _consts().items():
        m["c_" + k] = v
    return m


def run_rwkv(xT_b, w_in_l, prm, T=SEQ):
    nc = build_rwkv(T)
    in_maps = [rwkv_inmap(c // 4, c % 4, xT_b, w_in_l, prm) for c in range(NCORES)]
    res = run_bass_kernel_spmd(nc, in_maps, core_ids=list(range(NCORES)))
    out = np.zeros((BATCH, T, 256), np.float32)
    for c in range(NCORES):
        b, j = c // 4, c % 4
        out[b, :, 64 * j:64 * j + 64] = res.results[c]["orw"]
    return out


import math
import ml_dtypes
NEG = -30000.0
BIGNEG = -1.0e30
DCONST = 3072
RL = 3840


def _t5_bucket_np(rel):
    rel = np.maximum(rel, 0)
    rel_f = np.maximum(rel, 1).astype(np.float32)
    large = 16 + (np.log(rel_f / np.float32(16)) / np.float32(math.log(4096 / 16)) * np.float32(16)).astype(np.int32)
    large = np.minimum(large, 31)
    return np.where(rel < 16, rel, large)


def _moba_consts(T):
    c = {}
    c["ident"] = np.eye(128, dtype=np.float32)
    E = np.zeros((64, T), np.float32)
    for j in range(T // 256):
        E[j, j * 256:(j + 1) * 256] = 1.0
    c["E"] = E.astype(ml_dtypes.bfloat16)
    stair = np.zeros((128, 128), np.float32)
    stair[:, 64:] = BIGNEG
    c["stair"] = stair
    z = np.ones((128, 128), np.float32)
    z[:, 64] = 0.0
    c["zown"] = z
    return c


def _bias_table(rel_bias_h):
    p = np.arange(128)[:, None]
    m = np.arange(RL)[None, :]
    rel = m - p - 384
    tab = np.take(rel_bias_h.astype(np.float32), _t5_bucket_np(rel))
    return np.where(rel >= 0, tab, np.float32(NEG)).astype(np.float32)


def build_moba(T):
    nc = bass.Bass("TRN2", target_bir_lowering=False)
    dt = lambda n, s, d=F32: nc.dram_tensor(n, list(s), d, kind="ExternalInput").ap()
    ins = dict(xT=dt("xT", [D_MODEL, T]), wm=dt("wm", [D_MODEL, 512]),
               R2_0=dt("R2_0", [128, RL]), R2_1=dt("R2_1", [128, RL]), b31=dt("b31", [2]))
    cc = _moba_consts(T)
    cst = {k: dt("c_" + k, v.shape, BF16 if k == "E" else F32) for k, v in cc.items()}
    om = nc.dram_tensor("om", [T, 128], F32, kind="ExternalOutput").ap()
    P = Prog(nc)
    emit_moba(P, nc, ins, cst, om, T)
    P.finish()
    return nc


def emit_moba(P, nc, ins, cst, om, T):
    NG = T // TG
    NT = T // 128
    xT = ins["xT"]
    w_sb = P.sb("m_w", [128, KC, 512])
    x_sb = P.sb("m_x", [128, KC, TG])
    ident = P.sb("m_ident", [128, 128])
    stair = P.sb("m_stair", [128, 128])
    zown = P.sb("m_zown", [128, 128])
    R2 = [P.sb("m_R2_%d" % h, [128, RL]) for h in range(2)]
    b31 = P.sb("m_b31", [128, 2])
    Kaug = [P.sb("m_Kaug%d" % h, [128, T], BF16) for h in range(2)]
    Vaug = P.sb("m_Vaug", [128, NT, 2, 65], BF16)
    Qaug = [P.sb("m_Qaug%d" % h, [128, TG], BF16) for h in range(2)]
    q32 = [P.sb("m_q32_%d" % h, [64, TG]) for h in range(2)]
    k32 = P.sb("m_k32", [64, TG])
    kbar = [P.sb("m_kbar%d" % h, [64, 64]) for h in range(2)]
    g_sb = P.sb("m_g", [128, 4, 128])
    sg_sb = P.sb("m_sg", [128, 4, 128])
    gm = P.sb("m_gm", [128, 64])
    top8 = P.sb("m_top8", [128, 8])
    MM = [P.sb("m_MM%d" % h, [128, 128]) for h in range(2)]
    tmpS = [P.sb("m_tmpS%d" % i, [128, TG]) for i in range(2)]
    PT = [P.sb("m_PT%d" % i, [128, TG], BF16) for i in range(2)]
    rinv = P.sb("m_rinv", [128, 4])
    o_sb = [P.sb("m_o%d" % i, [128, 4, 128]) for i in range(2)]
    pbig = [P.ps("m_pb%d" % i, [128, TG]) for i in range(2)]
    pS = [P.ps("m_pS%d" % i, [128, TG]) for i in range(2)]
    pO = [P.ps("m_pO%d" % q, [128, 512]) for q in range(4)]

    for kc in range(KC):
        P.dma(("sp", "act")[kc % 2], w_sb[:, kc, :], ins["wm"][kc * 128:(kc + 1) * 128, :], writes=["w"])
    P.dma("sp", ident[:], cst["ident"], writes=["ident"])
    P.dma("sp", stair[:], cst["stair"], writes=["stair"])
    P.dma("sp", zown[:], cst["zown"], writes=["zown"])
    for h in range(2):
        P.dma("act", R2[h][:], ins["R2_%d" % h], writes=[("R2", h)])
        for q in range(4):
            P.dma("sp", Kaug[h][64:128, q * (T // 4):(q + 1) * (T // 4)], cst["E"][:, q * (T // 4):(q + 1) * (T // 4)],
                  writes=[("KaugE", h)])
        P.op("pool", (lambda e, h=h: e.memset(kbar[h][:], 0.0)), writes=[("kbar", h)])
        P.op("pool", (lambda e, h=h: e.memset(MM[h][:], 0.0)), writes=[("MM", h)])
    P.dma("sp", b31[:], ins["b31"].partition_broadcast(128), writes=["b31"])
    P.op("pool", lambda e: e.memset(Vaug[:], 1.0), writes=["Vaug"])

    nb = [0]

    def big():
        i = nb[0] % 2
        nb[0] += 1
        return pbig[i], ("pb", i)

    small = big
    ns = [0]
    for g in range(NG):
        load_x_group(P, xT, x_sb, g)
        gs = slice(g * TG, (g + 1) * TG)
        for h in range(2):
            pb, pk = big(); proj_group(P, x_sb, w_sb, "w", 64 * h, 64, pb, pk)
            P.op("dve", (lambda e, pb=pb, h=h: e.tensor_copy(out=q32[h][:], in_=pb[0:64, :])), reads=[pk], writes=[("q32", h)])
            P.op("act", (lambda e, h=h: e.copy(out=Qaug[h][0:64, :], in_=q32[h][:])), reads=[("q32", h)], writes=[("QaugQ", h)])
            pb, pk = big(); proj_group(P, x_sb, w_sb, "w", 128 + 64 * h, 64, pb, pk)
            P.op("dve", (lambda e, pb=pb: e.tensor_copy(out=k32[:], in_=pb[0:64, :])), reads=[pk], writes=["k32"])
            P.op("act", (lambda e, h=h, gs=gs: e.copy(out=Kaug[h][0:64, gs], in_=k32[:])), reads=["k32"], writes=[("KaugK", h)])
            P.op("dve", (lambda e, h=h, g=g: e.reduce_sum(out=kbar[h][:, 2 * g:2 * g + 2],
                                                          in_=k32[:].rearrange("p (b t) -> p b t", t=256), axis=AX.X)),
                 reads=["k32"], writes=[("kbar", h)])
        for tt in range(4):
            pb, pk = big()
            for kc in range(KC):
                P.op("pe", (lambda e, kc=kc, pb=pb, tt=tt: e.matmul(pb[:, 0:256], x_sb[:, kc, tt * 128:(tt + 1) * 128],
                                                                    w_sb[:, kc, 256:512], start=(kc == 0), stop=(kc == KC - 1))),
                     reads=[("x", kc), "w"], writes=[pk])
            kt = 4 * g + tt
            P.op("act", (lambda e, pb=pb, kt=kt: e.copy(out=Vaug[:, kt, :, 0:64],
                                                        in_=pb[:, 0:128].rearrange("p (h d) -> p h d", d=64))),
                 reads=[pk], writes=["Vaug"])
            P.op("dve", (lambda e, pb=pb, tt=tt: e.tensor_copy(out=g_sb[:, tt, :], in_=pb[:, 128:256])), reads=[pk], writes=["g"])
        P.op("act", lambda e: e.activation(out=sg_sb[:], in_=g_sb[:], func=AF.Sigmoid), reads=["g"], writes=["sg"])
        P.op("pool", lambda e: e.tensor_tensor(out=sg_sb[:], in0=sg_sb[:], in1=g_sb[:], op=ALU.mult), reads=["sg", "g"], writes=["sg"])
        for h in range(2):
            for qc in range(4):
                own = (4 * g + qc) // 2
                pg, pgk = small()
                P.op("pe", (lambda e, pg=pg, h=h, qc=qc: e.matmul(pg[:, 0:64], q32[h][:, qc * 128:(qc + 1) * 128], kbar[h][:],
                                                                  start=True, stop=True)),
                     reads=[("q32", h), ("kbar", h)], writes=[pgk])
                P.op("dve", (lambda e, pg=pg, own=own: e.tensor_tensor(out=gm[:], in0=pg[:, 0:64], in1=stair[:, 64 - own:128 - own],
                                                                      op=ALU.add)), reads=[pgk, "stair"], writes=["gm"])
                P.op("dve", lambda e: e.max(out=top8[:], in_=gm[:]), reads=["gm"], writes=["top8"])
                P.op("dve", (lambda e, h=h: e.tensor_scalar(out=MM[h][:, 64:128], in0=gm[:], scalar1=top8[:, 2:3], scalar2=float(NEG),
                                                            op0=ALU.is_lt, op1=ALU.mult)), reads=["gm", "top8"], writes=[("MM", h)])
                P.op("dve", (lambda e, h=h, own=own: e.tensor_tensor(out=MM[h][:, 64:128], in0=MM[h][:, 64:128],
                                                                    in1=zown[:, 64 - own:128 - own], op=ALU.mult)),
                     reads=[("MM", h), "zown"], writes=[("MM", h)])
                pt, ptk = small()
                P.op("pe", (lambda e, pt=pt, h=h: e.transpose(pt[:, 0:128], MM[h][:], ident[:])), reads=[("MM", h), "ident"], writes=[ptk])
                P.op("act", (lambda e, pt=pt, h=h, qc=qc: e.copy(out=Qaug[h][64:128, qc * 128:(qc + 1) * 128], in_=pt[64:128, 0:128])),
                     reads=[ptk], writes=[("QaugM", h)])
        ob = g % 2
        for h in range(2):
            nkt = 4 * g + 4
            for kt in range(nkt):
                D = g * TG - kt * 128
                si = ns[0] % 2
                ns[0] += 1
                P.op("pe", (lambda e, si=si, h=h, kt=kt: e.matmul(pS[si][:], Kaug[h][:, kt * 128:(kt + 1) * 128], Qaug[h][:],
                                                                  start=True, stop=True)),
                     reads=[("KaugK", h), ("KaugE", h), ("QaugQ", h), ("QaugM", h)], writes=[("pS", si)])
                if D >= DCONST:
                    P.op("act", (lambda e, si=si, h=h: e.activation(out=PT[si][:], in_=pS[si][:], func=AF.Exp, scale=HD ** -0.5,
                                                                    bias=b31[:, h:h + 1])),
                         reads=[("pS", si), "b31"], writes=[("PT", si)])
                else:
                    w0 = D + 384
                    P.op("dve", (lambda e, si=si, h=h, w0=w0: e.scalar_tensor_tensor(
                        out=tmpS[si][:], in0=pS[si][:], scalar=HD ** -0.5, in1=R2[h][:, w0:w0 + TG], op0=ALU.mult, op1=ALU.add)),
                        reads=[("pS", si), ("R2", h)], writes=[("tmpS", si)])
                    P.op("act", (lambda e, si=si: e.activation(out=PT[si][:], in_=tmpS[si][:], func=AF.Exp)),
                         reads=[("tmpS", si)], writes=[("PT", si)])
                for qc in range(4):
                    if 4 * g + qc < kt:
                        continue
                    P.op("pe", (lambda e, si=si, h=h, kt=kt, qc=qc, last=(kt == 4 * g + qc): e.matmul(
                        pO[qc][:, 0:65], PT[si][:, qc * 128:(qc + 1) * 128], Vaug[:, kt, h, :], start=(kt == 0), stop=last)),
                        reads=[("PT", si), "Vaug"], writes=[("pO", qc)])
            for qc in range(4):
                P.op("dve", (lambda e, qc=qc: e.reciprocal(out=rinv[:, qc:qc + 1], in_=pO[qc][:, 64:65])), reads=[("pO", qc)], writes=["rinv"])
                P.op("dve", (lambda e, h=h, qc=qc, ob=ob: e.scalar_tensor_tensor(
                    out=o_sb[ob][:, qc, 64 * h:64 * h + 64], in0=pO[qc][:, 0:64], scalar=rinv[:, qc:qc + 1],
                    in1=sg_sb[:, qc, 64 * h:64 * h + 64], op0=ALU.mult, op1=ALU.mult)),
                    reads=[("pO", qc), "rinv", "sg"], writes=[("o", ob)])
        P.dma("sp", om[gs, :].rearrange("(c p) f -> p c f", p=128), o_sb[ob][:], reads=[("o", ob)], writes=[("om", g)],
              is_output=True)


def moba_inmap(b, j, xT_b, w_in_l, rel_bias, T):
    base = 1040
    cols = np.concatenate([np.arange(base + 512 * i + 128 * j, base + 512 * i + 128 * j + 128) for i in range(4)])
    m = {"xT": xT_b[b], "wm": np.ascontiguousarray(w_in_l[:, cols]),
         "R2_0": _bias_table(rel_bias[:, 2 * j]), "R2_1": _bias_table(rel_bias[:, 2 * j + 1]),
         "b31": np.ascontiguousarray(rel_bias[31, 2 * j:2 * j + 2]).astype(np.float32)}
    for k, v in _moba_consts(T).items():
        m["c_" + k] = v
    return m


def run_moba(xT_b, w_in_l, rel_bias, T=SEQ):
    nc = build_moba(T)
    in_maps = [moba_inmap(c // 4, c % 4, xT_b, w_in_l, rel_bias, T) for c in range(NCORES)]
    res = run_bass_kernel_spmd(nc, in_maps, core_ids=list(range(NCORES)))
    out = np.zeros((BATCH, T, 512), np.float32)
    for c in range(NCORES):
        b, j = c // 4, c % 4
        out[b, :, 128 * j:128 * j + 128] = res.results[c]["om"]
    return out


_NC_CACHE = {}


def _get_nc(kind, builder, *a):
    key = (kind,) + tuple(a)
    if key not in _NC_CACHE:
        _NC_CACHE[key] = builder(*a)
    return _NC_CACHE[key]


def _launch(nc, in_maps):
    return run_bass_kernel_spmd(nc, in_maps, core_ids=list(range(NCORES))).results


def kernel(x, w_in, w_out, gla_a_up, gla_a_bias, gla_norm_w, moba_rel_bias, rwkv_mu, rwkv_w_up, rwkv_w0,
           rwkv_a_up, rwkv_a0, rwkv_k_k, rwkv_k_a, rwkv_r_k, rwkv_gn_w, rwkv_gn_b, ln_w, ln_b):
    f = lambda a: np.ascontiguousarray(np.asarray(a, dtype=np.float32))
    x = f(x)
    B, T, D = x.shape
    w_in, w_out = f(w_in), f(w_out)
    rel_bias = f(moba_rel_bias)
    cst = _consts()
    for l in range(DEPTH):
        xT_b = [np.ascontiguousarray(x[b].T) for b in range(B)]
        nc = _get_nc("gla", build_gla, T)
        maps = []
        for c in range(NCORES):
            b, j = c // 4, c % 4
            cols = np.concatenate([np.arange(256 * i + 64 * j, 256 * i + 64 * j + 64) for i in range(4)]
                                  + [np.arange(1024, 1040)])
            m = {"xT": xT_b[b], "wg": np.ascontiguousarray(w_in[l][:, cols]),
                 "a_up": np.ascontiguousarray(f(gla_a_up)[l][:, 64 * j:64 * j + 64]),
                 "a_bias": np.ascontiguousarray(f(gla_a_bias)[l][64 * j:64 * j + 64].reshape(64, 1)),
                 "norm_w": f(gla_norm_w)[l]}
            for k, v in cst.items():
                m["c_" + k] = v
            maps.append(m)
        r_gla = _launch(nc, maps)
        nc = _get_nc("moba", build_moba, T)
        maps = [moba_inmap(c // 4, c % 4, xT_b, w_in[l], rel_bias, T) for c in range(NCORES)]
        r_moba = _launch(nc, maps)
        nc = _get_nc("rwkv", build_rwkv, T)
        prm = {"rwkv_mu": f(rwkv_mu)[l], "rwkv_w_up": f(rwkv_w_up)[l], "rwkv_w0": f(rwkv_w0)[l],
               "rwkv_a_up": f(rwkv_a_up)[l], "rwkv_a0": f(rwkv_a0)[l], "rwkv_k_k": f(rwkv_k_k)[l],
               "rwkv_k_a": f(rwkv_k_a)[l], "rwkv_r_k": f(rwkv_r_k)[l], "rwkv_gn_w": f(rwkv_gn_w)[l],
               "rwkv_gn_b": f(rwkv_gn_b)[l]}
        maps = [rwkv_inmap(c // 4, c % 4, xT_b, w_in[l], prm) for c in range(NCORES)]
        r_rwkv = _launch(nc, maps)
        o = np.empty((B, T, D), np.float32)
        for c in range(NCORES):
            b, j = c // 4, c % 4
            o[b, :, 64 * j:64 * j + 64] = r_gla[c]["og"]
            o[b, :, 256 + 128 * j:256 + 128 * j + 128] = r_moba[c]["om"]
            o[b, :, 768 + 64 * j:768 + 64 * j + 64] = r_rwkv[c]["orw"]
        NT = B * T // NCORES
        nc = _get_nc("B", build_B, NT)
        o2 = o.reshape(B * T, D)
        x2 = x.reshape(B * T, D)
        maps = []
        for c in range(NCORES):
            sl = slice(c * NT, (c + 1) * NT)
            maps.append({"oT": np.ascontiguousarray(o2[sl].T), "xres": np.ascontiguousarray(x2[sl]),
                         "w_out": w_out[l], "lnw": f(ln_w)[l], "lnb": f(ln_b)[l]})
        r_b = _launch(nc, maps)
        x = np.concatenate([r["y"] for r in r_b], axis=0).reshape(B, T, D)
    return x
```
